# Optimizing a Trainium2 kernel written in Bass

```python
import math
import jax, jax.numpy as jnp
from jax import lax
import numpy as np

D_MODEL = 1024
BATCH = 2
SEQ = 8192
DEPTH = 2

MEM_LEN = 256
D_MIX = D_MODEL
SSD_WIDTH = D_MIX // 2
SSD_HEAD_DIM = 64
SSD_HEADS = SSD_WIDTH // SSD_HEAD_DIM
SSD_GROUPS = 2
SSD_HEADS_PER_GROUP = SSD_HEADS // SSD_GROUPS
SSD_STATE = 128
SSD_CONV = 4
SSD_CHUNK = 128
SSD_XBC = SSD_WIDTH + 2 * SSD_GROUPS * SSD_STATE

S5_WIDTH = D_MIX // 4
S5_GROUP_CH = 16
S5_GROUPS = S5_WIDTH // S5_GROUP_CH
S5_STATE = 64

RG_WIDTH = D_MIX - SSD_WIDTH - S5_WIDTH
RG_BLOCKS = 4
RG_BLOCK_DIM = RG_WIDTH // RG_BLOCKS
RG_CONV = 4
RG_C = 8.0

XA_HEADS = 4
XA_HEAD_DIM = D_MODEL // XA_HEADS
D_FF = 4 * D_MODEL

ALPHA = (2.0 * DEPTH) ** 0.25
BETA = (8.0 * DEPTH) ** -0.25
LN_EPS = 1e-5

IN_COLS = (SSD_WIDTH, SSD_XBC, SSD_HEADS, S5_WIDTH, RG_WIDTH, RG_WIDTH)
D_IN = SSD_WIDTH + SSD_XBC + SSD_HEADS + S5_WIDTH + RG_WIDTH + RG_WIDTH

kernel_name = "hybrid_ssd_s5_rglru_deepnorm"


def layer_norm(x, g, b):
    x32 = x.astype(jnp.float32)
    mu = jnp.mean(x32, axis=-1, keepdims=True)
    var = jnp.mean(jnp.square(x32 - mu), axis=-1, keepdims=True)
    return (x32 - mu) * lax.rsqrt(var + LN_EPS) * g.astype(jnp.float32) + b.astype(jnp.float32)


def causal_dwconv(x, w, b):
    k, c = w.shape
    y = lax.conv_general_dilated(x, w[:, None, :].astype(x.dtype), window_strides=(1,),
                                 padding=[(k - 1, 0)], dimension_numbers=('NWC', 'WIO', 'NWC'),
                                 feature_group_count=c)
    return y + b.astype(x.dtype)


def _lin_combine(left, right):
    a1, b1 = left
    a2, b2 = right
    return a1 * a2, a2 * b1 + b2


def linear_scan(a, b):
    return lax.associative_scan(_lin_combine, (a, b), axis=1)[1]


def ssd_mixer(z, xbc, dt_raw, conv_w, conv_b, dt_bias, a_log, d_skip, norm_w):
    bsz, seq, _ = z.shape
    nc = seq // SSD_CHUNK
    xbc = jax.nn.silu(causal_dwconv(xbc, conv_w.astype(jnp.float32), conv_b.astype(jnp.float32)))
    xs = xbc[..., :SSD_WIDTH]
    bm = xbc[..., SSD_WIDTH:SSD_WIDTH + SSD_GROUPS * SSD_STATE]
    cm = xbc[..., SSD_WIDTH + SSD_GROUPS * SSD_STATE:]
    dt = jax.nn.softplus(dt_raw + dt_bias.astype(jnp.float32))
    a = -jnp.exp(a_log.astype(jnp.float32))
    g, hg, p, n, q = SSD_GROUPS, SSD_HEADS_PER_GROUP, SSD_HEAD_DIM, SSD_STATE, SSD_CHUNK
    xh = xs.reshape(bsz, seq, SSD_HEADS, p)
    xdt = (xh * dt[..., None]).reshape(bsz, nc, q, g, hg, p)
    adt = (dt * a).reshape(bsz, nc, q, g, hg).transpose(0, 1, 3, 4, 2)
    bc = bm.reshape(bsz, nc, q, g, n)
    cc = cm.reshape(bsz, nc, q, g, n)
    a_cs = jnp.cumsum(adt, axis=-1)
    mask = jnp.tril(jnp.ones((q, q), dtype=bool))
    seg = a_cs[..., :, None] - a_cs[..., None, :]
    lmat = jnp.exp(jnp.where(mask, seg, -jnp.inf))
    cb = jnp.einsum('bclgn,bcsgn->bcgls', cc, bc)
    y_diag = jnp.einsum('bcgls,bcghls,bcsghp->bclghp', cb, lmat, xdt)
    decay_states = jnp.exp(a_cs[..., -1:] - a_cs)
    states = jnp.einsum('bclgn,bcghl,bclghp->bcghpn', bc, decay_states, xdt)
    chunk_decay = jnp.exp(a_cs[..., -1])

    def step(s, inp):
        st, dec = inp
        return s * dec[..., None, None] + st, s

    init = jnp.zeros((bsz, g, hg, p, n), jnp.float32)
    _, prev = lax.scan(step, init, (jnp.moveaxis(states, 1, 0), jnp.moveaxis(chunk_decay, 1, 0)))
    prev = jnp.moveaxis(prev, 0, 1)
    y_off = jnp.einsum('bclgn,bcghpn,bcghl->bclghp', cc, prev, jnp.exp(a_cs))
    y = (y_diag + y_off).reshape(bsz, seq, SSD_HEADS, p) + xh * d_skip.astype(jnp.float32)[:, None]
    y = y.reshape(bsz, seq, SSD_WIDTH) * jax.nn.silu(z)
    y = y * lax.rsqrt(jnp.mean(jnp.square(y), axis=-1, keepdims=True) + LN_EPS)
    return y * norm_w.astype(jnp.float32)


def s5_mixer(u, lam_re, lam_im, log_step, b_re, b_im, c_re, c_im, d_skip, glu_w, glu_b):
    bsz, seq, _ = u.shape
    f32 = jnp.float32
    ug = u.reshape(bsz, seq, S5_GROUPS, S5_GROUP_CH).astype(jnp.complex64)
    lam = lax.complex(lam_re.astype(f32), lam_im.astype(f32))
    step = jnp.exp(log_step.astype(f32))[:, None]
    lam_bar = jnp.exp(lam * step)
    bmat = lax.complex(b_re.astype(f32), b_im.astype(f32))
    b_bar = ((lam_bar - 1.0) / lam)[..., None] * bmat
    bu = jnp.einsum('gpc,blgc->blgp', b_bar, ug)
    h = linear_scan(jnp.broadcast_to(lam_bar, bu.shape), bu)
    cmat = lax.complex(c_re.astype(f32), c_im.astype(f32))
    y = jnp.real(jnp.einsum('gcp,blgp->blgc', cmat, h)).reshape(bsz, seq, S5_WIDTH)
    y = jax.nn.gelu(y + d_skip.astype(f32) * u)
    return y * jax.nn.sigmoid(jnp.einsum('blc,ce->ble', y, glu_w.astype(f32)) + glu_b.astype(f32))


def rglru_mixer(xr, gate_in, conv_w, conv_b, wa, ba, wx, bx, lam):
    bsz, seq, _ = xr.shape
    f32 = jnp.float32
    xc = causal_dwconv(xr, conv_w.astype(f32), conv_b.astype(f32))
    xh = xc.reshape(bsz, seq, RG_BLOCKS, RG_BLOCK_DIM)
    r = jax.nn.sigmoid(jnp.einsum('blhi,hij->blhj', xh, wa.astype(f32)) + ba.astype(f32)).reshape(bsz, seq, RG_WIDTH)
    i = jax.nn.sigmoid(jnp.einsum('blhi,hij->blhj', xh, wx.astype(f32)) + bx.astype(f32)).reshape(bsz, seq, RG_WIDTH)
    log_a = -RG_C * r * jax.nn.softplus(-lam.astype(f32))
    a = jnp.exp(log_a)
    mult = jnp.sqrt(-jnp.expm1(2.0 * log_a))
    h = linear_scan(a, mult * (i * xc))
    return h * jax.nn.gelu(gate_in)


def cross_attention(x, mem, wq, wk, wv, wo):
    bsz, seq, _ = x.shape
    f32 = jnp.float32
    q = jnp.einsum('bld,de->ble', x, wq.astype(f32)).reshape(bsz, seq, XA_HEADS, XA_HEAD_DIM)
    k = jnp.einsum('bmd,de->bme', mem, wk.astype(f32)).reshape(bsz, -1, XA_HEADS, XA_HEAD_DIM)
    v = jnp.einsum('bmd,de->bme', mem, wv.astype(f32)).reshape(bsz, -1, XA_HEADS, XA_HEAD_DIM)
    s = jnp.einsum('blhd,bmhd->bhlm', q, k) * (1.0 / math.sqrt(XA_HEAD_DIM))
    pr = jax.nn.softmax(s, axis=-1)
    o = jnp.einsum('bhlm,bmhd->blhd', pr, v).reshape(bsz, seq, D_MODEL)
    return jnp.einsum('ble,ed->bld', o, wo.astype(f32))


def squared_relu_mlp(x, w1, w2):
    hdn = jnp.square(jax.nn.relu(jnp.einsum('bld,df->blf', x, w1.astype(jnp.float32))))
    return jnp.einsum('blf,fd->bld', hdn, w2.astype(jnp.float32))


def setup_inputs(seed: int = 0) -> dict:
    key = jax.random.key(seed)
    ks = iter(jax.random.split(key, 64))
    f32 = jnp.float32

    def nrm(shape, scale):
        return jax.random.normal(next(ks), shape, f32) * scale

    def uni(shape, lo, hi):
        return jax.random.uniform(next(ks), shape, f32, lo, hi)

    L = DEPTH
    x = nrm((BATCH, SEQ, D_MODEL), 1.0)
    mem = nrm((BATCH, MEM_LEN, D_MODEL), 1.0)
    dt0 = jnp.exp(uni((L, SSD_HEADS), math.log(1e-3), math.log(1e-1)))
    a_rg = uni((L, RG_WIDTH), 0.9, 0.999) ** (1.0 / RG_C)
    n_idx = jnp.arange(S5_STATE, dtype=f32)
    return {
        "x": x,
        "mem": mem,
        "w_in": nrm((L, D_MODEL, D_IN), D_MODEL ** -0.5),
        "w_out": nrm((L, D_MIX, D_MODEL), BETA * D_MIX ** -0.5),
        "ssd_conv_w": nrm((L, SSD_CONV, SSD_XBC), SSD_CONV ** -0.5),
        "ssd_conv_b": nrm((L, SSD_XBC), 0.02),
        "ssd_dt_bias": dt0 + jnp.log(-jnp.expm1(-dt0)),
        "ssd_a_log": jnp.log(uni((L, SSD_HEADS), 1.0, 16.0)),
        "ssd_d": 1.0 + nrm((L, SSD_HEADS), 0.02),
        "ssd_norm_w": 1.0 + nrm((L, SSD_WIDTH), 0.02),
        "s5_lam_re": -0.5 + nrm((L, S5_GROUPS, S5_STATE), 0.01),
        "s5_lam_im": jnp.pi * n_idx + nrm((L, S5_GROUPS, S5_STATE), 0.01),
        "s5_log_step": uni((L, S5_GROUPS), math.log(1e-3), math.log(1e-1)),
        "s5_b_re": nrm((L, S5_GROUPS, S5_STATE, S5_GROUP_CH), (2.0 * S5_GROUP_CH) ** -0.5),
        "s5_b_im": nrm((L, S5_GROUPS, S5_STATE, S5_GROUP_CH), (2.0 * S5_GROUP_CH) ** -0.5),
        "s5_c_re": nrm((L, S5_GROUPS, S5_GROUP_CH, S5_STATE), (2.0 * S5_STATE) ** -0.5),
        "s5_c_im": nrm((L, S5_GROUPS, S5_GROUP_CH, S5_STATE), (2.0 * S5_STATE) ** -0.5),
        "s5_d": nrm((L, S5_WIDTH), 1.0),
        "s5_glu_w": nrm((L, S5_WIDTH, S5_WIDTH), S5_WIDTH ** -0.5),
        "s5_glu_b": nrm((L, S5_WIDTH), 0.02),
        "rg_conv_w": nrm((L, RG_CONV, RG_WIDTH), RG_CONV ** -0.5),
        "rg_conv_b": nrm((L, RG_WIDTH), 0.02),
        "rg_wa": nrm((L, RG_BLOCKS, RG_BLOCK_DIM, RG_BLOCK_DIM), RG_BLOCK_DIM ** -0.5),
        "rg_ba": nrm((L, RG_BLOCKS, RG_BLOCK_DIM), 0.02),
        "rg_wx": nrm((L, RG_BLOCKS, RG_BLOCK_DIM, RG_BLOCK_DIM), RG_BLOCK_DIM ** -0.5),
        "rg_bx": nrm((L, RG_BLOCKS, RG_BLOCK_DIM), 0.02),
        "rg_lambda": jnp.log(a_rg / (1.0 - a_rg)),
        "ln1_g": 1.0 + nrm((L, D_MODEL), 0.02),
        "ln1_b": nrm((L, D_MODEL), 0.02),
        "xa_wq": nrm((L, D_MODEL, D_MODEL), D_MODEL ** -0.5),
        "xa_wk": nrm((L, D_MODEL, D_MODEL), D_MODEL ** -0.5),
        "xa_wv": nrm((L, D_MODEL, D_MODEL), BETA * D_MODEL ** -0.5),
        "xa_wo": nrm((L, D_MODEL, D_MODEL), BETA * D_MODEL ** -0.5),
        "ln2_g": 1.0 + nrm((L, D_MODEL), 0.02),
        "ln2_b": nrm((L, D_MODEL), 0.02),
        "mlp_w1": nrm((L, D_MODEL, D_FF), BETA * D_MODEL ** -0.5),
        "mlp_w2": nrm((L, D_FF, D_MODEL), BETA * D_FF ** -0.5),
        "ln3_g": 1.0 + nrm((L, D_MODEL), 0.02),
        "ln3_b": nrm((L, D_MODEL), 0.02),
    }


def reference(x, mem, w_in, w_out, ssd_conv_w, ssd_conv_b, ssd_dt_bias, ssd_a_log, ssd_d, ssd_norm_w,
              s5_lam_re, s5_lam_im, s5_log_step, s5_b_re, s5_b_im, s5_c_re, s5_c_im, s5_d, s5_glu_w, s5_glu_b,
              rg_conv_w, rg_conv_b, rg_wa, rg_ba, rg_wx, rg_bx, rg_lambda, ln1_g, ln1_b,
              xa_wq, xa_wk, xa_wv, xa_wo, ln2_g, ln2_b, mlp_w1, mlp_w2, ln3_g, ln3_b):
    f32 = jnp.float32
    out_dtype = x.dtype
    h = x.astype(f32)
    memf = mem.astype(f32)
    split_idx = []
    acc = 0
    for w in IN_COLS[:-1]:
        acc += w
        split_idx.append(acc)
    for l in range(DEPTH):
        proj = jnp.einsum('bld,dk->blk', h, w_in[l].astype(f32))
        z, xbc, dt_raw, u_s5, x_rg, g_rg = jnp.split(proj, split_idx, axis=-1)
        y_ssd = ssd_mixer(z, xbc, dt_raw, ssd_conv_w[l], ssd_conv_b[l], ssd_dt_bias[l], ssd_a_log[l],
                          ssd_d[l], ssd_norm_w[l])
        y_s5 = s5_mixer(u_s5, s5_lam_re[l], s5_lam_im[l], s5_log_step[l], s5_b_re[l], s5_b_im[l],
                        s5_c_re[l], s5_c_im[l], s5_d[l], s5_glu_w[l], s5_glu_b[l])
        y_rg = rglru_mixer(x_rg, g_rg, rg_conv_w[l], rg_conv_b[l], rg_wa[l], rg_ba[l], rg_wx[l], rg_bx[l],
                           rg_lambda[l])
        y = jnp.concatenate([y_ssd, y_s5, y_rg], axis=-1)
        y = jnp.einsum('ble,ed->bld', y, w_out[l].astype(f32))
        h = layer_norm(ALPHA * h + y, ln1_g[l], ln1_b[l])
        h = layer_norm(ALPHA * h + cross_attention(h, memf, xa_wq[l], xa_wk[l], xa_wv[l], xa_wo[l]),
                       ln2_g[l], ln2_b[l])
        h = layer_norm(ALPHA * h + squared_relu_mlp(h, mlp_w1[l], mlp_w2[l]), ln3_g[l], ln3_b[l])
    return h.astype(out_dtype)
```

```python
import math
import numpy as np
from contextlib import ExitStack
import concourse.bass as bass
import concourse.mybir as mybir
from concourse.bass_utils import run_bass_kernel_spmd

F32 = mybir.dt.float32
BF16 = mybir.dt.bfloat16
AF = mybir.ActivationFunctionType
ALU = mybir.AluOpType

D = 1024
T = 2048
NCH = 16
TB = 256
NB = T // TB
DEPTH = 2
ALPHA = (2.0 * DEPTH) ** 0.25
EPS = 1e-5
DIN = 2312
C_Z, C_XBC, C_DT, C_U, C_XR, C_GR = 0, 512, 1536, 1544, 1800, 2056
TWO_PI = 2.0 * math.pi
HAL = 8


def _box(ap):
    pairs = ap.ap
    off = int(ap.offset)
    space = str(ap.space)
    if space in ("SB", "PSUM"):
        pstep = pairs[0][0]
        if pstep == 0:
            p0, f0 = 0, off
        else:
            p0, f0 = off // pstep, off % pstep
        p1 = p0 + pairs[0][1]
        ext = 1
        for st, cnt in pairs[1:]:
            ext += abs(st) * (cnt - 1)
        f1 = f0 + ext
        if space == "PSUM":
            epb = 512 if ap.dtype == F32 else 1024
            f0 = (f0 // epb) * epb
            f1 = -(-f1 // epb) * epb
            p0, p1 = 0, 128
        return (ap.name, p0, p1, f0, f1)
    ext = 1
    for st, cnt in pairs:
        ext += abs(st) * (cnt - 1)
    return (ap.name, 0, 1, off, off + ext)


class _Dummy:
    def then_inc(self, *a, **k):
        return self


class Sched:
    stopped = False

    def __init__(self, nc, es, n_dma_sems=32):
        Sched.stopped = False
        self.nc = nc
        self.eng = {"pe": nc.tensor, "act": nc.scalar, "dve": nc.vector, "pool": nc.gpsimd, "sp": nc.sync}
        self.sem = {k: es.enter_context(nc.semaphore("s_" + k)) for k in ("pe", "act", "dve", "pool")}
        self.seq = {k: 0 for k in self.sem}
        self.dsem = [es.enter_context(nc.semaphore(f"s_dma{i}")) for i in range(n_dma_sems)]
        self.dgen = [0] * n_dma_sems
        self.dnext = {"sp": 0, "pool": 0}
        self.dhalf = n_dma_sems // 2
        self.waited = {k: {} for k in self.eng}
        self.recs = {}
        self.n_inst = 0
        self.n_wait = 0

    @staticmethod
    def _overlap(a, b):
        return a[1] < b[2] and b[1] < a[2] and a[3] < b[4] and b[3] < a[4]

    @staticmethod
    def _covers(a, b):
        return a[1] <= b[1] and a[2] >= b[2] and a[3] <= b[3] and a[4] >= b[4]

    def _deps(self, ek, reads, writes):
        need = {}
        for aps, isw in ((reads, False), (writes, True)):
            for ap in aps:
                bx = _box(ap)
                psum = str(ap.space) == "PSUM"
                for r in self.recs.get(bx[0], ()):
                    if (isw or r[4] or (psum and r[5] != ek)) and self._overlap(bx, r[0]):
                        if ek == "pe" and r[5] == "pe":
                            continue
                        if need.get(r[1], (None, 0))[1] < r[3]:
                            need[r[1]] = (r[2], r[3])
        return need

    def _record(self, ek, reads, writes, semname, sem, val):
        for aps, isw in ((reads, False), (writes, True)):
            for ap in aps:
                bx = _box(ap)
                lst = self.recs.setdefault(bx[0], [])
                if isw:
                    lst[:] = [r for r in lst if not self._covers(bx, r[0])]
                elif ek != "dma":
                    lst[:] = [r for r in lst if not (r[5] == ek and not r[4] and self._covers(bx, r[0]))]
                lst.append([bx, semname, sem, val, isw, ek])

    def _emit_waits(self, ek, need):
        e = self.eng[ek]
        for key, (sem, val) in need.items():
            if self.waited[ek].get(key, 0) < val:
                e.wait_ge(sem, val)
                self.waited[ek][key] = val
                self.n_wait += 1

    def op(self, ek, fn, reads=(), writes=(), signal=True):
        if Sched.stopped:
            return _Dummy()
        need = self._deps(ek, reads, writes)
        self._emit_waits(ek, need)
        ins = fn(self.eng[ek])
        self.n_inst += 1
        if signal:
            self.seq[ek] += 1
            ins.then_inc(self.sem[ek], 1)
            val = self.seq[ek]
        else:
            val = self.seq[ek] + 1
        self._record(ek, reads, writes, "s_" + ek, self.sem[ek], val)
        return ins

    def dma(self, out, in_, q="sp", **kw):
        if Sched.stopped:
            return _Dummy()
        need = self._deps(q, [in_], [out])
        i = self.dnext[q] + (0 if q == "sp" else self.dhalf)
        self.dnext[q] = (self.dnext[q] + 1) % self.dhalf
        sem = self.dsem[i]
        if self.dgen[i] > 0:
            need["s_dma%d" % i] = (sem, 16 * self.dgen[i])
        self._emit_waits(q, need)
        self.dgen[i] += 1
        val = 16 * self.dgen[i]
        ins = self.eng[q].dma_start(out=out, in_=in_, **kw)
        ins.then_inc(sem, 16)
        self.n_inst += 1
        self._record("dma", [in_], [out], "s_dma%d" % i, sem, val)
        return ins

    def all_pending(self):
        need = {}
        for lst in self.recs.values():
            for r in lst:
                if need.get(r[1], (None, 0))[1] < r[3]:
                    need[r[1]] = (r[2], r[3])
        for k in self.sem:
            if self.seq[k] > 0:
                need["s_" + k] = (self.sem[k], self.seq[k])
        for i, sm in enumerate(self.dsem):
            if self.dgen[i] > 0:
                need["s_dma%d" % i] = (sm, 16 * self.dgen[i])
        return need

    def barrier(self):
        need = self.all_pending()
        for ek in self.eng:
            self._emit_waits(ek, dict(need))
        self.recs = {}

    def finish(self):
        self._emit_waits("sp", self.all_pending())


class K:
    def __init__(self, S):
        self.S = S

    @staticmethod
    def _aps(*xs):
        return [x for x in xs if x is not None and not isinstance(x, (int, float))]

    def tt(self, ek, out, in0, in1, op):
        return self.S.op(ek, lambda e: e.tensor_tensor(out, in0, in1, op), reads=[in0, in1], writes=[out])

    def ts(self, ek, out, in0, s1, s2, op0, op1=None):
        if op1 is None:
            return self.S.op(ek, lambda e: e.tensor_scalar(out, in0, s1, None, op0), reads=self._aps(in0, s1), writes=[out])
        return self.S.op(ek, lambda e: e.tensor_scalar(out, in0, s1, s2, op0, op1), reads=self._aps(in0, s1, s2), writes=[out])

    def stt(self, ek, out, in0, sc, in1, op0, op1):
        ek = "dve"
        return self.S.op(ek, lambda e: e.scalar_tensor_tensor(out, in0, sc, in1, op0, op1), reads=self._aps(in0, sc, in1), writes=[out])

    def copy(self, ek, out, in_):
        if ek == "act":
            return self.S.op(ek, lambda e: e.copy(out, in_), reads=[in_], writes=[out])
        return self.S.op(ek, lambda e: e.tensor_copy(out, in_), reads=[in_], writes=[out])

    def act(self, out, in_, func, bias=None, scale=None, accum_out=None):
        kw = {}
        if bias is not None:
            kw["bias"] = bias
        if scale is not None:
            kw["scale"] = scale
        if accum_out is not None:
            kw["accum_out"] = accum_out
        w = [out] + ([accum_out] if accum_out is not None else [])
        return self.S.op("act", lambda e: e.activation(out, in_, func, **kw), reads=self._aps(in_, bias, scale), writes=w)

    def mm(self, out, lhsT, rhs, start=True, stop=True):
        return self.S.op("pe", lambda e: e.matmul(out, lhsT, rhs, start=start, stop=stop), reads=[lhsT, rhs], writes=[out], signal=stop)

    def tr(self, out, in_, ident, signal=True):
        return self.S.op("pe", lambda e: e.transpose(out, in_, ident), reads=[in_, ident], writes=[out], signal=signal)

    def scan(self, ek, out, d0, d1, init, op0=ALU.mult, op1=ALU.add):
        ek = "dve"
        return self.S.op(ek, lambda e: e.tensor_tensor_scan(out, d0, d1, init, op0, op1), reads=self._aps(d0, d1, init), writes=[out])

    def memset(self, ek, out, val):
        return self.S.op(ek, lambda e: e.memset(out, val), reads=[], writes=[out])


class StopBuild(Exception):
    pass


def chk(n):
    if build_program.stop == n:
        Sched.stopped = True


def build_program(mode):
    full = mode == "B"
    nc = bass.Bass("TRN2", target_bir_lowering=False)

    def din(name, shape, dt=F32):
        return nc.dram_tensor(name, list(shape), dt, kind="ExternalInput").ap()

    def dout(name, shape, dt=F32):
        return nc.dram_tensor(name, list(shape), dt, kind="ExternalOutput").ap()

    def dscr(name, shape, dt=F32):
        return nc.dram_tensor(name, list(shape), dt, kind="Internal").ap()

    h_in = din("h", [T, D])
    halo_in = din("halo", [128, D])
    w_in = din("w_in", [D, DIN])
    cst = din("cst", [128, 4, 128])
    convw = din("convw", [128, 8, 4])
    convb = din("convb", [128, 8])
    rowp = din("rowp", [3, 8])
    s5pp = din("s5pp", [128, 3, 8])
    s5row = din("s5row", [3, 1024])
    s5bpad = din("s5bpad", [128, 2, 8, 128])
    rgcw = din("rgcw", [128, 2, 4])
    rgp = din("rgp", [128, 4, 2])
    rgw = din("rgw", [128, 2, 2, 128])
    if full:
        w_out = din("w_out", [D, D])
        normw = din("normw", [512])
        s5cpad = din("s5cpad", [128, 2, 8, 128])
        s5dg = din("s5dg", [128, 2, 2])
        gluw = din("gluw", [256, 256])
        lnp = din("lnp", [6, D])
        mem_in = din("mem", [256, D])
        wq = din("wq", [D, D])
        wk = din("wk", [D, D])
        wv = din("wv", [D, D])
        wo = din("wo", [D, D])
        w1 = din("w1", [D, 4 * D])
        w2 = din("w2", [4 * D, D])
        p_ssd = din("p_ssd", [3, 128, 512])
        p_asum = din("p_asum", [3, 8])
        p_s5 = din("p_s5", [3, 128, 2, 8])
        p_rg = din("p_rg", [3, 128, 2, 2])
        out_h = dout("out_h", [T, D])
        h1_d = dscr("h1_d", [T, D])
        h1T_d = dscr("h1T_d", [128, 8, T], BF16)
        h2_d = dscr("h2_d", [T, D])
        h2T_d = dscr("h2T_d", [128, 8, T], BF16)
    else:
        o_ssd = dout("o_ssd", [128, 512])
        o_asum = dout("o_asum", [1, 8])
        o_s5 = dout("o_s5", [128, 2, 8])
        o_rg = dout("o_rg", [128, 2, 2])

    with ExitStack() as es:
        S = Sched(nc, es)
        k = K(S)

        def sb(name, shape, dt=F32, stack=es):
            return stack.enter_context(nc.sbuf_tensor(name, list(shape), dt))

        def ps(name, shape, dt=F32):
            return es.enter_context(nc.psum_tensor(name, list(shape), dt))

        gbanks = [ps(f"pg{i}", [128, 512]) for i in range(3)]
        pbT = ps("pbT", [128, 1024], BF16)
        pbT2 = ps("pbT2", [128, 1024], BF16)
        pb5 = ps("pb5", [128, 512])
        pA = ps("pA", [128, 1024])
        gctr = [0]

        def gb():
            b = gbanks[gctr[0] % 3]
            gctr[0] += 1
            return b

        cst_f = sb("cst_f", [128, 4, 128])
        S.dma(cst_f[:], cst[:])
        ident = sb("ident", [128, 128], BF16)
        k.copy("dve", ident[:], cst_f[:, 0, :])
        tri = cst_f[:, 1, :]
        maskneg = cst_f[:, 2, :]
        iota_j = cst_f[:, 3, :]
        ones_b = sb("ones_b", [128, 128], BF16)
        k.memset("pool", ones_b[:], 1.0)
        tri_b = sb("tri_b", [128, 128], BF16)
        k.copy("dve", tri_b[:], tri)

        try:
          with ExitStack() as ms:
              def msb(name, shape, dt=F32):
                  return sb(name, shape, dt, stack=ms)

              ST = msb("ST", [128, 512])
              STb = msb("STb", [128, 512], BF16)
              gin_re = msb("gin_re", [128, 8])
              gin_im = msb("gin_im", [128, 8])
              rgc = msb("rgc", [128, 2])
              asum = msb("asum", [128, 8])
              sumr = msb("sumr", [128, 2])
              tcos = msb("tcos", [128, 8, 128])
              tsin = msb("tsin", [128, 8, 128])
              wre_b = msb("wre_b", [128, 8, 128], BF16)
              wim_b = msb("wim_b", [128, 8, 128], BF16)
              hT = msb("hT", [128, 8, HAL + TB], BF16)
              cneg = msb("cneg", [128, 2])
              cneg2 = msb("cneg2", [128, 2])
              a_bc = msb("a_bc", [128, 8])
              s5pp_t = msb("s5pp_t", [128, 3, 8])
              p_stp = msb("p_stp", [128, 8])[:]
              th_p = msb("th_p", [128, 8])[:]
              rr_p = msb("rr_p", [128, 8])[:]
              c128 = msb("c128", [128, 8])[:]
              s128 = msb("s128", [128, 8])[:]
              win_b = msb("win_b", [128, 8, DIN], BF16)
              convw_t = msb("convw_t", [128, 8, 4])
              convb_t = msb("convb_t", [128, 8])
              rgcw_t = msb("rgcw_t", [128, 2, 4])
              rgp_t = msb("rgp_t", [128, 4, 2])
              rgw_b = msb("rgw_b", [128, 2, 2, 128], BF16)
              rowp_bc = msb("rowp_bc", [128, 3, 8])
              if full:
                  cre_b = msb("cre_b", [128, 8, 128], BF16)
                  cimn_b = msb("cimn_b", [128, 8, 128], BF16)
                  wout_b = msb("wout_b", [128, 8, D], BF16)
                  gluw_b = msb("gluw_b", [128, 2, 256], BF16)
                  normw_bc = msb("normw_bc", [128, 512])
                  lng = msb("lng", [128, D])
                  lnb = msb("lnb", [128, D])
                  s5dg_t = msb("s5dg_t", [128, 2, 2])

              w_in_v = w_in.rearrange("(k p) n -> p k n", p=128)
              for kk in range(8):
                  S.dma(win_b[:, kk, :], w_in_v[:, kk, :], q="pool")
              if full:
                  w_out_v = w_out.rearrange("(k p) n -> p k n", p=128)
                  for kk in range(8):
                      S.dma(wout_b[:, kk, :], w_out_v[:, kk, :], q="pool")
                  S.dma(gluw_b[:], gluw.rearrange("(k p) n -> p k n", p=128), q="pool")
                  S.dma(normw_bc[:], normw.partition_broadcast(128))
                  S.dma(lng[:], lnp[0].partition_broadcast(128))
                  S.dma(lnb[:], lnp[1].partition_broadcast(128))
                  S.dma(s5dg_t[:], s5dg[:])
              S.dma(convw_t[:], convw[:])
              S.dma(convb_t[:], convb[:])
              S.dma(rgcw_t[:], rgcw[:])
              S.dma(rgp_t[:], rgp[:])
              S.dma(rgw_b[:], rgw[:], q="pool")
              for i in range(3):
                  S.dma(rowp_bc[:, i, :], rowp[i].partition_broadcast(128))
              S.dma(s5pp_t[:], s5pp[:])
              dtb_bc = rowp_bc[:, 0, :]
              dsk_bc = rowp_bc[:, 2, :]
              k.act(a_bc[:], rowp_bc[:, 1, :], AF.Exp)
              k.ts("dve", a_bc[:], a_bc[:], -1.0, None, ALU.mult)
              k.act(cneg[:], rgp_t[:, 3, :], AF.Exp, scale=-1.0)
              k.act(cneg[:], cneg[:], AF.Ln, bias=1.0)
              k.ts("dve", cneg2[:], cneg[:], -16.0, None, ALU.mult)
              k.ts("dve", cneg[:], cneg[:], -8.0, None, ALU.mult)

              with ExitStack() as tstack:
                  def tsb(name, shape, dt=F32):
                      return sb(name, shape, dt, stack=tstack)

                  def s5_params(lr, li, ls, stp, th, rr):
                      k.act(stp, ls, AF.Exp)
                      k.tt("dve", th, li, stp, ALU.mult)
                      k.tt("dve", rr, lr, stp, ALU.mult)
                      k.act(rr, rr, AF.Exp)

                  rr_tmp = {}

                  def range_reduce(out, x, shape, pfx):
                      key = pfx + "rrt"
                      if key not in rr_tmp:
                          rr_tmp[key] = tsb(key, shape)[:]
                      t = rr_tmp[key]
                      MAGIC = 12582912.0
                      k.ts("dve", t, x, 1.0 / TWO_PI, None, ALU.mult)
                      k.ts("dve", t, t, MAGIC, None, ALU.add)
                      k.ts("dve", t, t, -MAGIC, None, ALU.add)
                      k.stt("dve", out, t, -6.28125, x, ALU.mult, ALU.add)
                      k.stt("dve", out, t, -(TWO_PI - 6.28125), out, ALU.mult, ALU.add)
                      k.ts("dve", out, out, -3.1415925, 3.1415925, ALU.max, ALU.min)

                  def sincos(th, shape, pfx, mult, c, s):
                      a = tsb(pfx + "a", shape)[:]
                      a2 = tsb(pfx + "a2", shape)[:]
                      k.ts("dve", a, th, mult, None, ALU.mult)
                      range_reduce(a2, a, shape, pfx)
                      k.act(s, a2, AF.Sin)
                      k.ts("dve", a, a, 0.5 * math.pi, None, ALU.add)
                      range_reduce(a2, a, shape, pfx)
                      k.act(c, a2, AF.Sin)

                  if full:
                      cpad_f = tsb("cpad_f", [128, 2, 8, 128])
                      S.dma(cpad_f[:], s5cpad[:])
                      k.copy("dve", cre_b[:], cpad_f[:, 0])
                      k.ts("dve", cimn_b[:], cpad_f[:, 1], -1.0, None, ALU.mult)
                  s5_params(s5pp_t[:, 0, :], s5pp_t[:, 1, :], s5pp_t[:, 2, :], p_stp, th_p, rr_p)
                  sincos(th_p, [128, 8], "p128", 128.0, c128, s128)
                  targ = tsb("targ", [128, 128])
                  targ2 = tsb("targ2", [128, 128])
                  for q in range(8):
                      k.ts("dve", targ[:], iota_j, th_p[:, q:q + 1], None, ALU.mult)
                      range_reduce(targ2[:], targ[:], [128, 128], "tab")
                      k.act(tsin[:, q, :], targ2[:], AF.Sin)
                      k.ts("dve", targ[:], targ[:], 0.5 * math.pi, None, ALU.add)
                      range_reduce(targ2[:], targ[:], [128, 128], "tab")
                      k.act(tcos[:, q, :], targ2[:], AF.Sin)
                  R1 = [128, 1024]
                  s5row_t = tsb("s5row_t", [128, 3, 1024])
                  for i in range(3):
                      S.dma(s5row_t[:, i, :], s5row[i].partition_broadcast(128))
                  lr_r, li_r = s5row_t[:, 0, :], s5row_t[:, 1, :]
                  r_stp = tsb("r_stp", R1)[:]
                  th_r = tsb("th_r", R1)[:]
                  rr_r = tsb("rr_r", R1)[:]
                  s5_params(lr_r, li_r, s5row_t[:, 2, :], r_stp, th_r, rr_r)
                  c_r = tsb("c_r", R1)[:]
                  s_r = tsb("s_r", R1)[:]
                  sincos(th_r, R1, "r1", 1.0, c_r, s_r)
                  xx = tsb("xx", R1)
                  yy = tsb("yy", R1)
                  den = r_stp
                  tmp_r = th_r
                  k.tt("dve", xx[:], rr_r, c_r, ALU.mult)
                  k.ts("dve", xx[:], xx[:], -1.0, None, ALU.add)
                  k.tt("dve", yy[:], rr_r, s_r, ALU.mult)
                  k.tt("dve", den, lr_r, lr_r, ALU.mult)
                  k.tt("dve", tmp_r, li_r, li_r, ALU.mult)
                  k.tt("dve", den, den, tmp_r, ALU.add)
                  S.op("dve", lambda e: e.reciprocal(den, den), reads=[den], writes=[den])
                  cr = c_r
                  ci = s_r
                  k.tt("dve", cr, xx[:], lr_r, ALU.mult)
                  k.tt("dve", tmp_r, yy[:], li_r, ALU.mult)
                  k.tt("dve", cr, cr, tmp_r, ALU.add)
                  k.tt("dve", cr, cr, den, ALU.mult)
                  k.tt("dve", ci, yy[:], lr_r, ALU.mult)
                  k.tt("dve", tmp_r, xx[:], li_r, ALU.mult)
                  k.tt("dve", ci, ci, tmp_r, ALU.subtract)
                  k.tt("dve", ci, ci, den, ALU.mult)
                  bpad_f = tsb("bpad_f", [128, 2, 8, 128])
                  S.dma(bpad_f[:], s5bpad[:])
                  bre_v = bpad_f[:, 0].rearrange("p q c -> p (q c)")
                  bim_v = bpad_f[:, 1].rearrange("p q c -> p (q c)")
                  k.tt("dve", xx[:], cr, bre_v, ALU.mult)
                  k.tt("dve", yy[:], ci, bim_v, ALU.mult)
                  k.tt("dve", wre_b[:].rearrange("p q c -> p (q c)"), xx[:], yy[:], ALU.subtract)
                  k.tt("dve", xx[:], cr, bim_v, ALU.mult)
                  k.tt("dve", yy[:], ci, bre_v, ALU.mult)
                  k.tt("dve", wim_b[:].rearrange("p q c -> p (q c)"), xx[:], yy[:], ALU.add)

                  if full:
                      pa = tsb("pa", [128, 3, 8])
                      for i in range(3):
                          S.dma(pa[:, i, :], p_asum[i].partition_broadcast(128))
                      k.act(pa[:], pa[:], AF.Exp)
                      pst = tsb("pst", [128, 2, 512])
                      S.dma(ST[:], p_ssd[2])
                      S.dma(pst[:, 0, :], p_ssd[1])
                      S.dma(pst[:, 1, :], p_ssd[0])
                      v3 = lambda t: t.rearrange("p (h d) -> p h d", d=64)
                      k.tt("dve", v3(ST[:]), v3(ST[:]), pa[:, 1, :].unsqueeze(2).to_broadcast([128, 8, 64]), ALU.mult)
                      k.tt("dve", ST[:], ST[:], pst[:, 0, :], ALU.add)
                      k.tt("dve", v3(ST[:]), v3(ST[:]), pa[:, 0, :].unsqueeze(2).to_broadcast([128, 8, 64]), ALU.mult)
                      k.tt("dve", ST[:], ST[:], pst[:, 1, :], ALU.add)
                      k.copy("act", STb[:], ST[:])
                      L_r = tsb("L_r", [128, 8])
                      k.act(L_r[:], rr_p, AF.Ln)
                      k.act(L_r[:], L_r[:], AF.Exp, scale=float(T))
                      cT = tsb("cT", [128, 8])[:]
                      sT = tsb("sT", [128, 8])[:]
                      sincos(th_p, [128, 8], "pT", float(T), cT, sT)
                      L_re = tsb("L_re", [128, 8])
                      L_im = tsb("L_im", [128, 8])
                      k.tt("dve", L_re[:], L_r[:], cT, ALU.mult)
                      k.tt("dve", L_im[:], L_r[:], sT, ALU.mult)
                      ps5 = tsb("ps5", [128, 3, 2, 8])
                      S.dma(ps5[:], p_s5.rearrange("i p c q -> p i c q"))
                      t1 = tsb("t1", [128, 8])
                      t2 = tsb("t2", [128, 8])
                      k.copy("dve", gin_re[:], ps5[:, 2, 0, :])
                      k.copy("dve", gin_im[:], ps5[:, 2, 1, :])
                      for i in (1, 0):
                          k.tt("dve", t1[:], gin_re[:], L_re[:], ALU.mult)
                          k.tt("dve", t2[:], gin_im[:], L_im[:], ALU.mult)
                          k.tt("dve", t1[:], t1[:], t2[:], ALU.subtract)
                          k.tt("dve", t2[:], gin_re[:], L_im[:], ALU.mult)
                          k.tt("dve", gin_im[:], gin_im[:], L_re[:], ALU.mult)
                          k.tt("dve", gin_im[:], gin_im[:], t2[:], ALU.add)
                          k.tt("dve", gin_im[:], gin_im[:], ps5[:, i, 1, :], ALU.add)
                          k.tt("dve", gin_re[:], t1[:], ps5[:, i, 0, :], ALU.add)
                      prg = tsb("prg", [128, 3, 2, 2])
                      S.dma(prg[:], p_rg.rearrange("i p c m -> p i c m"))
                      drg = tsb("drg", [128, 3, 2])
                      for i in range(3):
                          k.tt("dve", drg[:, i, :], prg[:, i, 1, :], cneg[:], ALU.mult)
                      k.act(drg[:], drg[:], AF.Exp)
                      k.copy("dve", rgc[:], prg[:, 2, 0, :])
                      for i in (1, 0):
                          k.tt("dve", rgc[:], rgc[:], drg[:, i, :], ALU.mult)
                          k.tt("dve", rgc[:], rgc[:], prg[:, i, 0, :], ALU.add)
                  else:
                      k.memset("dve", ST[:], 0.0)
                      k.memset("dve", STb[:], 0.0)
                      k.memset("dve", gin_re[:], 0.0)
                      k.memset("dve", gin_im[:], 0.0)
                      k.memset("dve", rgc[:], 0.0)
                      k.memset("dve", asum[:], 0.0)
                      k.memset("dve", sumr[:], 0.0)
                  S.barrier()
              chk(1)

              hb = [msb(f"hb{i}", [128, D], BF16) for i in range(2)]
              S.dma(hb[1][:], halo_in[:], q="pool")
              for kk in range(8):
                  k.tr(pbT[:, kk * 128:(kk + 1) * 128], hb[1][:, kk * 128:(kk + 1) * 128], ident[:], signal=(kk == 7))
              k.copy("dve", hT[:, :, 0:HAL], pbT[:].rearrange("p (k c) -> p k c", c=128)[:, :, 128 - HAL:128])
              chk(2)

              xraw = msb("xraw", [128, 8, HAL + TB])
              xrraw = msb("xrraw", [128, 2, HAL + TB])
              xbcT = msb("xbcT", [128, 8, TB], BF16)
              cacc = [msb(f"cacc{i}", [128, TB]) for i in range(2)]
              uT32 = msb("uT32", [128, 2, TB])
              uTb = msb("uTb", [128, 2, TB], BF16)
              if full:
                  ymixT = msb("ymixT", [128, 8, TB], BF16)
                  hreT = msb("hreT", [128, 8, TB], BF16)
                  himT = msb("himT", [128, 8, TB], BF16)
                  grg = msb("grg", [128, 2, TB])
              dtp = msb("dtp", [128, 8])
              dt_t = msb("dt_t", [128, 8])
              adt = msb("adt", [128, 8])
              adt3 = msb("adt3", [128, 3, 8], BF16)
              adtr = msb("adtr", [128, 8])
              adtb = msb("adtb", [128, 3, 8, 128], BF16)
              acs = msb("acs", [128, 16])
              negacs = msb("negacs", [128, 8])
              dec = msb("dec", [128, 8])
              cd = msb("cd", [128, 8])
              xdt = msb("xdt", [128, 512], BF16)
              xdtd = msb("xdtd", [128, 512], BF16)
              Btok = msb("Btok", [128, 256], BF16)
              bre = msb("bre", [128, 4, 128])
              bim = msb("bim", [128, 4, 128])
              seg = msb("seg", [128, 8, 128])
              Erow = msb("Erow", [128, 8, 128])
              q1 = seg[:, 0:4, :]
              q2 = seg[:, 4:8, :]
              bpr = Erow[:, 0:4, :]
              bpi = Erow[:, 4:8, :]
              bigtmp = msb("bigtmp", [128, 4, D])
              gre = msb("gre", [128, 8, 128])
              gim = msb("gim", [128, 8, 128])
              e1 = msb("e1", [128, 8])
              e2 = msb("e2", [128, 8])
              if full:
                  xs_tok = msb("xs_tok", [128, 512], BF16)
                  MT = msb("MT", [128, 8, 128], BF16)
                  cbt_sb = msb("cbt_sb", [128, 256])
                  Cexp = msb("Cexp", [128, 8, 128], BF16)
                  sz = bigtmp[:, 0, 0:512]
                  ysk = bigtmp[:, 0, 512:1024]
                  y1 = bigtmp[:, 1, 0:512]
                  junk = bigtmp[:, 1, 512:1024]
                  ss = msb("ss", [128, 1])
                  rstd = msb("rstd", [128, 1])
                  ynb = msb("ynb", [128, 512], BF16)
                  y0 = bigtmp[:, 2, 0:2 * TB].rearrange("p (m t) -> p m t", m=2)
                  y1g = bigtmp[:, 2, 2 * TB:4 * TB].rearrange("p (m t) -> p m t", m=2)
                  y1gb = msb("y1gb", [128, 2, TB], BF16)
                  sg = bigtmp[:, 3, 0:TB]
                  hres = [bigtmp[:, 3, :], bigtmp[:, 3, :]]
                  vv = bigtmp[:, 0, :]
                  stats = msb("stats", [128, 2, 6])
                  mv = msb("mv", [128, 2])
                  nmr = msb("nmr", [128, 1])
                  xn = bigtmp[:, 1, :]
                  h1t = bigtmp[:, 2, :]
                  h1tb = msb("h1tb", [128, D], BF16)
                  h1Tt = msb("h1Tt", [128, 8, 128], BF16)
              xc = msb("xc", [128, TB])
              xcb = msb("xcb", [128, TB], BF16)
              r_t = msb("r_t", [128, TB])
              i_t = msb("i_t", [128, TB])
              a_t = msb("a_t", [128, TB])
              a2_t = msb("a2_t", [128, TB])
              hrg = msb("hrg", [128, TB])
              racc = msb("racc", [128, 1])

              def bc3(ap2, n):
                  return ap2.unsqueeze(2).to_broadcast([128, ap2.shape[1], n])

              conv_chunks_xbc = [(C_XBC + c * 128) for c in range(8)]

              for b in range(NB):
                  if b > 0:
                      k.copy("dve", hT[:, :, 0:HAL], hT[:, :, TB:TB + HAL])
                  for cc in range(TB // 128):
                      i = b * (TB // 128) + cc
                      hbt = hb[i % 2]
                      S.dma(hbt[:], h_in[i * 128:(i + 1) * 128, :], q="pool")
                      pt = pbT if i % 2 == 0 else pbT2
                      for kk in range(8):
                          k.tr(pt[:, kk * 128:(kk + 1) * 128], hbt[:, kk * 128:(kk + 1) * 128], ident[:], signal=(kk == 7))
                      k.copy("dve" if i % 2 == 0 else "act", hT[:, :, HAL + cc * 128:HAL + (cc + 1) * 128],
                             pt[:].rearrange("p (k c) -> p k c", c=128))
                  chk(21)
                  grp = [("xbc", c, conv_chunks_xbc[c]) for c in range(8)] + [("u", m, C_U + m * 128) for m in range(2)] + \
                        [("xr", m, C_XR + m * 128) for m in range(2)]
                  if full:
                      grp += [("gr", m, C_GR + m * 128) for m in range(2)]
                  for gi, (kind, c, col) in enumerate(grp):
                      bank = gb()
                      conv = kind in ("xbc", "xr")
                      c0 = 0 if conv else HAL
                      nn = HAL + TB - c0
                      for kk in range(8):
                          k.mm(bank[:, 0:nn], win_b[:, kk, col:col + 128], hT[:, kk, c0:HAL + TB], start=(kk == 0), stop=(kk == 7))
                      if gi == 0:
                          chk(22)
                      if kind == "xbc":
                          k.copy("act", xraw[:, c, :], bank[:, 0:nn])
                          if gi == 0:
                              chk(23)
                      elif kind == "u":
                          k.copy("act", uT32[:, c, :], bank[:, 0:TB])
                          k.copy("pool", uTb[:, c, :], uT32[:, c, :])
                      elif kind == "xr":
                          k.copy("act", xrraw[:, c, :], bank[:, 0:nn])
                      else:
                          k.act(grg[:, c, :], bank[:, 0:TB], AF.Gelu_apprx_tanh)
                      if gi == 7:
                          chk(24)
                      if gi == 9:
                          chk(25)
                      if gi == 11:
                          chk(26)
                  chk(3)
                  for c in range(8):
                      eng = "dve" if c % 2 == 0 else "pool"
                      acc = cacc[c % 2]
                      k.ts(eng, acc[:], xraw[:, c, HAL - 3:HAL - 3 + TB], convw_t[:, c, 0:1], convb_t[:, c:c + 1], ALU.mult, ALU.add)
                      for j in range(1, 4):
                          k.stt(eng, acc[:], xraw[:, c, HAL - 3 + j:HAL - 3 + j + TB], convw_t[:, c, j:j + 1], acc[:], ALU.mult, ALU.add)
                      k.act(xbcT[:, c, :], acc[:], AF.Silu)
                  chk(4)
                  for cc in range(TB // 128):
                      ci = b * (TB // 128) + cc
                      o = cc * 128
                      t0 = HAL + cc * 128
                      for kk in range(8):
                          k.mm(pb5[:, 0:8], hT[:, kk, t0:t0 + 128], win_b[:, kk, C_DT:C_DT + 8], start=(kk == 0), stop=(kk == 7))
                      k.tt("dve", dtp[:], pb5[:, 0:8], dtb_bc, ALU.add)
                      k.act(dtp[:], dtp[:], AF.Exp)
                      k.act(dt_t[:], dtp[:], AF.Ln, bias=1.0)
                      k.tt("dve", adt[:], dt_t[:], a_bc[:], ALU.mult)
                      k.copy("dve", adt3[:, 0, :], adt[:])
                      k.tt("dve", adtr[:], adt[:], adt3[:, 0, :], ALU.subtract)
                      k.copy("dve", adt3[:, 1, :], adtr[:])
                      k.tt("dve", adtr[:], adtr[:], adt3[:, 1, :], ALU.subtract)
                      k.copy("dve", adt3[:, 2, :], adtr[:])
                      for j3 in range(3):
                          k.mm(pb5[:, 8:16], tri_b[:], adt3[:, j3, :], start=(j3 == 0), stop=(j3 == 2))
                      for j3 in range(3):
                          k.mm(pb5[:, 16:24], ones_b[:], adt3[:, j3, :], start=(j3 == 0), stop=(j3 == 2))
                      k.copy("dve", acs[:], pb5[:, 8:24])
                      k.tt("dve", dec[:], acs[:, 8:16], acs[:, 0:8], ALU.subtract)
                      k.act(dec[:], dec[:], AF.Exp)
                      k.act(cd[:], acs[:, 8:16], AF.Exp)
                      if not full:
                          k.tt("dve", asum[:], asum[:], acs[:, 8:16], ALU.add)
                      chk(41)
                      for c in range(6):
                          k.tr(pbT[:, c * 128:(c + 1) * 128], xbcT[:, c, o:o + 128], ident[:], signal=(c == 5))
                      pxs = pbT[:, 0:512].rearrange("p (h d) -> p h d", d=64)
                      k.tt("dve", xdt[:].rearrange("p (h d) -> p h d", d=64), pxs, bc3(dt_t[:], 64), ALU.mult)
                      k.tt("pool", xdtd[:].rearrange("p (h d) -> p h d", d=64), xdt[:].rearrange("p (h d) -> p h d", d=64), bc3(dec[:], 64), ALU.mult)
                      k.copy("act", Btok[:], pbT[:, 512:768])
                      chk(42)
                      if full:
                          k.copy("act", xs_tok[:], pbT[:, 0:512])
                          k.ts("dve", negacs[:], acs[:, 0:8], -1.0, None, ALU.mult)
                          k.copy("pool", adtb[:], adt3[:].unsqueeze(3).to_broadcast([128, 3, 8, 128]))
                          for hh in range(8):
                              for j3 in range(3):
                                  k.mm(pA[:, hh * 128:(hh + 1) * 128], adtb[:, j3, hh, :], tri_b[:], start=(j3 == 0), stop=(j3 == 2))
                          for g in range(2):
                              k.mm(pb5[:, 32 + g * 128:32 + (g + 1) * 128], xbcT[:, 4 + g, o:o + 128], xbcT[:, 6 + g, o:o + 128])
                          for hh in range(8):
                              k.stt("dve", seg[:, hh, :], pA[:, hh * 128:(hh + 1) * 128], negacs[:, hh:hh + 1], maskneg, ALU.add, ALU.add)
                          k.act(seg[:], seg[:], AF.Exp)
                          k.copy("act", cbt_sb[:], pb5[:, 32:288])
                          cbt = cbt_sb[:].rearrange("p (g l) -> p g l", g=2).unsqueeze(2).to_broadcast([128, 2, 4, 128])
                          k.tt("dve", MT[:].rearrange("p (g h) l -> p g h l", g=2), seg[:].rearrange("p (g h) l -> p g h l", g=2), cbt, ALU.mult)
                          k.act(Erow[:], pA[:].rearrange("p (h l) -> p h l", h=8), AF.Exp)
                          ctb = xbcT[:, 6:8, o:o + 128].unsqueeze(2).to_broadcast([128, 2, 4, 128])
                          k.tt("pool", Cexp[:].rearrange("p (g h) l -> p g h l", g=2), Erow[:].rearrange("p (g h) l -> p g h l", g=2), ctb, ALU.mult)
                          ybank = gb()
                          for hh in range(8):
                              k.mm(ybank[:, hh * 64:(hh + 1) * 64], MT[:, hh, :], xdt[:, hh * 64:(hh + 1) * 64], start=True, stop=False)
                              k.mm(ybank[:, hh * 64:(hh + 1) * 64], Cexp[:, hh, :], STb[:, hh * 64:(hh + 1) * 64], start=False, stop=True)
                      chk(43)
                      stp = gb()
                      for g in range(2):
                          k.mm(stp[:, g * 256:(g + 1) * 256], Btok[:, g * 128:(g + 1) * 128], xdtd[:, g * 256:(g + 1) * 256])
                      k.tt("dve", ST[:].rearrange("p (h d) -> p h d", d=64), ST[:].rearrange("p (h d) -> p h d", d=64), bc3(cd[:], 64), ALU.mult)
                      k.tt("dve", ST[:], ST[:], stp[:], ALU.add)
                      k.copy("act", STb[:], ST[:])
                      if full:
                          zb = gb()
                          for kk in range(8):
                              k.mm(zb[:], hT[:, kk, t0:t0 + 128], win_b[:, kk, C_Z:C_Z + 512], start=(kk == 0), stop=(kk == 7))
                          k.act(sz[:], zb[:], AF.Silu)
                          k.tt("pool", ysk[:].rearrange("p (h d) -> p h d", d=64), xs_tok[:].rearrange("p (h d) -> p h d", d=64), bc3(dsk_bc, 64), ALU.mult)
                          k.tt("dve", y1[:], ybank[:], ysk[:], ALU.add)
                          k.tt("dve", y1[:], y1[:], sz[:], ALU.mult)
                          k.act(junk[:], y1[:], AF.Square, accum_out=ss[:])
                          k.ts("dve", rstd[:], ss[:], 1.0 / 512.0, EPS, ALU.mult, ALU.add)
                          k.act(rstd[:], rstd[:], AF.Ln)
                          k.act(rstd[:], rstd[:], AF.Exp, scale=-0.5)
                          k.stt("dve", ynb[:], y1[:], rstd[:, 0:1], normw_bc[:], ALU.mult, ALU.mult)
                          for c in range(4):
                              k.tr(pbT2[:, c * 128:(c + 1) * 128], ynb[:, c * 128:(c + 1) * 128], ident[:], signal=(c == 3))
                          k.copy("act", ymixT[:, 0:4, o:o + 128], pbT2[:, 0:512].rearrange("p (c l) -> p c l", c=4))
                      chk(5)
                      for half in range(2):
                          for qi in range(4):
                              q = half * 4 + qi
                              k.mm(pA[:, qi * 128:(qi + 1) * 128], wre_b[:, q, :], uTb[:, half, o:o + 128])
                              k.mm(pA[:, 512 + qi * 128:512 + (qi + 1) * 128], wim_b[:, q, :], uTb[:, half, o:o + 128])
                          k.copy("act", bre[:], pA[:, 0:512].rearrange("p (q l) -> p q l", q=4))
                          k.copy("act", bim[:], pA[:, 512:1024].rearrange("p (q l) -> p q l", q=4))
                          tc4 = tcos[:, half * 4:(half + 1) * 4, :]
                          ts4 = tsin[:, half * 4:(half + 1) * 4, :]
                          k.tt("dve", q1[:], bre[:], tc4, ALU.mult)
                          k.tt("pool", q2[:], bim[:], ts4, ALU.mult)
                          k.tt("dve", bpr[:], q1[:], q2[:], ALU.add)
                          k.tt("pool", q1[:], bim[:], tc4, ALU.mult)
                          k.tt("dve", q2[:], bre[:], ts4, ALU.mult)
                          k.tt("pool", bpi[:], q1[:], q2[:], ALU.subtract)
                          for qi in range(4):
                              q = half * 4 + qi
                              k.scan("dve", gre[:, q, :], rr_p[:, q:q + 1].to_broadcast([128, 128]), bpr[:, qi, :], gin_re[:, q:q + 1])
                              k.scan("pool", gim[:, q, :], rr_p[:, q:q + 1].to_broadcast([128, 128]), bpi[:, qi, :], gin_im[:, q:q + 1])
                      if full:
                          hre_o = hreT[:, :, o:o + 128]
                          him_o = himT[:, :, o:o + 128]
                          k.tt("dve", seg[:], tcos[:], gre[:], ALU.mult)
                          k.tt("pool", Erow[:], tsin[:], gim[:], ALU.mult)
                          k.tt("dve", hre_o, seg[:], Erow[:], ALU.subtract)
                          k.tt("pool", seg[:], tcos[:], gim[:], ALU.mult)
                          k.tt("dve", Erow[:], tsin[:], gre[:], ALU.mult)
                          k.tt("pool", him_o, seg[:], Erow[:], ALU.add)
                      glr = gre[:, :, 127]
                      gli = gim[:, :, 127]
                      k.tt("dve", e1[:], glr, c128, ALU.mult)
                      k.tt("dve", e2[:], gli, s128, ALU.mult)
                      k.tt("dve", gin_re[:], e1[:], e2[:], ALU.subtract)
                      k.tt("dve", e1[:], gli, c128, ALU.mult)
                      k.tt("dve", e2[:], glr, s128, ALU.mult)
                      k.tt("dve", gin_im[:], e1[:], e2[:], ALU.add)
                  chk(6)
                  for m in range(2):
                      k.ts("dve", xc[:], xrraw[:, m, HAL - 3:HAL - 3 + TB], rgcw_t[:, m, 0:1], rgp_t[:, 0, m:m + 1], ALU.mult, ALU.add)
                      for j in range(1, 4):
                          k.stt("dve", xc[:], xrraw[:, m, HAL - 3 + j:HAL - 3 + j + TB], rgcw_t[:, m, j:j + 1], xc[:], ALU.mult, ALU.add)
                      k.copy("act", xcb[:], xc[:])
                      ba_ = gb()
                      k.mm(ba_[:, 0:TB], rgw_b[:, 0, m, :], xcb[:])
                      bx_ = gb()
                      k.mm(bx_[:, 0:TB], rgw_b[:, 1, m, :], xcb[:])
                      k.act(r_t[:], ba_[:, 0:TB], AF.Sigmoid, bias=rgp_t[:, 1, m:m + 1], accum_out=racc[:])
                      k.act(i_t[:], bx_[:, 0:TB], AF.Sigmoid, bias=rgp_t[:, 2, m:m + 1])
                      if not full:
                          k.tt("dve", sumr[:, m:m + 1], sumr[:, m:m + 1], racc[:], ALU.add)
                      k.act(a_t[:], r_t[:], AF.Exp, scale=cneg[:, m:m + 1])
                      k.act(a2_t[:], r_t[:], AF.Exp, scale=cneg2[:, m:m + 1])
                      k.ts("dve", a2_t[:], a2_t[:], -1.0, 1.0, ALU.mult, ALU.add)
                      k.act(a2_t[:], a2_t[:], AF.Sqrt)
                      k.tt("pool", i_t[:], i_t[:], xc[:], ALU.mult)
                      k.tt("pool", i_t[:], i_t[:], a2_t[:], ALU.mult)
                      k.scan("dve", hrg[:], a_t[:], i_t[:], rgc[:, m:m + 1])
                      k.copy("dve", rgc[:, m:m + 1], hrg[:, TB - 1:TB])
                      if full:
                          k.tt("dve", ymixT[:, 6 + m, :], hrg[:], grg[:, m, :], ALU.mult)
                  if not full:
                      continue
                  chk(7)
                  for m in range(2):
                      bank = gb()
                      n = 0
                      for q in range(4 * m, 4 * m + 4):
                          k.mm(bank[:, 0:TB], cre_b[:, q, :], hreT[:, q, :], start=(n == 0), stop=False)
                          n += 1
                          k.mm(bank[:, 0:TB], cimn_b[:, q, :], himT[:, q, :], start=False, stop=(n == 7))
                          n += 1
                      k.stt("dve", y0[:, m, :], uT32[:, m, :], s5dg_t[:, 0, m:m + 1], bank[:, 0:TB], ALU.mult, ALU.add)
                      k.act(y1g[:, m, :], y0[:, m, :], AF.Gelu_apprx_tanh)
                      k.copy("pool", y1gb[:, m, :], y1g[:, m, :])
                  for m2 in range(2):
                      bank = gb()
                      for kc in range(2):
                          k.mm(bank[:, 0:TB], gluw_b[:, kc, m2 * 128:(m2 + 1) * 128], y1gb[:, kc, :], start=(kc == 0), stop=(kc == 1))
                      k.act(sg[:], bank[:, 0:TB], AF.Sigmoid, bias=s5dg_t[:, 1, m2:m2 + 1])
                      k.tt("dve", ymixT[:, 4 + m2, :], y1g[:, m2, :], sg[:], ALU.mult)
                  chk(8)
                  for cc in range(TB // 128):
                      ci = b * (TB // 128) + cc
                      o = cc * 128
                      hr = hres[ci % 2]
                      S.dma(hr[:], h_in[ci * 128:(ci + 1) * 128, :])
                      for half in range(2):
                          bank = gb()
                          for kk in range(8):
                              k.mm(bank[:], ymixT[:, kk, o:o + 128], wout_b[:, kk, half * 512:(half + 1) * 512], start=(kk == 0), stop=(kk == 7))
                          k.stt("dve", vv[:, half * 512:(half + 1) * 512], hr[:, half * 512:(half + 1) * 512], ALPHA, bank[:], ALU.mult, ALU.add)
                      layer_norm_tile(S, k, vv, stats, mv, rstd, nmr, xn, lng, lnb, h1t)
                      S.dma(h1_d[ci * 128:(ci + 1) * 128, :], h1t[:])
                      k.copy("act", h1tb[:], h1t[:])
                      for kk in range(8):
                          k.tr(pbT2[:, kk * 128:(kk + 1) * 128], h1tb[:, kk * 128:(kk + 1) * 128], ident[:], signal=(kk == 7))
                      k.copy("dve", h1Tt[:], pbT2[:].rearrange("p (k c) -> p k c", c=128))
                      S.dma(h1T_d[:, :, ci * 128:(ci + 1) * 128], h1Tt[:])

              if not full:
                  S.dma(o_ssd[:], ST[:])
                  S.dma(o_asum[:], asum[0:1, :])
                  s5o = msb("s5o", [128, 2, 8])
                  k.copy("dve", s5o[:, 0, :], gin_re[:])
                  k.copy("dve", s5o[:, 1, :], gin_im[:])
                  S.dma(o_s5[:], s5o[:])
                  rgo = msb("rgo", [128, 2, 2])
                  k.copy("dve", rgo[:, 0, :], rgc[:])
                  k.copy("dve", rgo[:, 1, :], sumr[:])
                  S.dma(o_rg[:], rgo[:])
              S.barrier()

        except StopBuild:
            S.barrier()
            S.finish()
            build_program.stats = (S.n_inst, S.n_wait)
            return nc

        if full:
            STAGE = build_program.stage
            def ln_out_tile(xs_, bt, hres_t, banks, gt, bt_, out_dram, outT_dram, ci, htb, hTt_):
                vv_ = bt[:, 0, :]
                xn_ = bt[:, 1, :]
                ho_ = bt[:, 2, :]
                for half in range(2):
                    k.stt("dve", vv_[:, half * 512:(half + 1) * 512], hres_t[:, half * 512:(half + 1) * 512], ALPHA,
                          banks[half][:], ALU.mult, ALU.add)
                layer_norm_tile(S, k, vv_, xs_["stats"], xs_["mv"], xs_["rstd"], xs_["nmr"], xn_, gt, bt_, ho_)
                S.dma(out_dram[ci * 128:(ci + 1) * 128, :], ho_)
                if outT_dram is not None:
                    k.copy("act", htb[:], ho_)
                    for kk in range(8):
                        k.tr(pbT2[:, kk * 128:(kk + 1) * 128], htb[:, kk * 128:(kk + 1) * 128], ident[:], signal=(kk == 7))
                    k.copy("dve", hTt_[:], pbT2[:].rearrange("p (k c) -> p k c", c=128))
                    S.dma(outT_dram[:, :, ci * 128:(ci + 1) * 128], hTt_[:])

            if STAGE >= 2:
                with ExitStack() as xs:
                    def xsb(name, shape, dt=F32):
                        return sb(name, shape, dt, stack=xs)
                    wq_b = xsb("wq_b", [128, 8, D], BF16)
                    wk_b = xsb("wk_b", [128, 8, D], BF16)
                    wv_b = xsb("wv_b", [128, 8, D], BF16)
                    wo_b = xsb("wo_b", [128, 8, D], BF16)
                    for wt_, wd_ in ((wk_b, wk), (wv_b, wv), (wq_b, wq), (wo_b, wo)):
                        wv_ = wd_.rearrange("(k p) n -> p k n", p=128)
                        for kk in range(0, 8, 2):
                            S.dma(wt_[:, kk:kk + 2, :], wv_[:, kk:kk + 2, :], q="pool")
                    lng2 = xsb("lng2", [128, D])
                    lnb2 = xsb("lnb2", [128, D])
                    S.dma(lng2[:], lnp[2].partition_broadcast(128))
                    S.dma(lnb2[:], lnp[3].partition_broadcast(128))
                    memb = xsb("memb", [128, 2, D], BF16)
                    S.dma(memb[:], mem_in.rearrange("(t p) d -> p t d", p=128), q="pool")
                    memT = xsb("memT", [128, 8, 256], BF16)
                    KT = xsb("KT", [128, 8, 256], BF16)
                    Vb = xsb("Vb", [128, 2, D], BF16)
                    for mt in range(2):
                        for kk in range(8):
                            k.tr(pbT[:, kk * 128:(kk + 1) * 128], memb[:, mt, kk * 128:(kk + 1) * 128], ident[:], signal=(kk == 7))
                        k.copy("dve", memT[:, :, mt * 128:(mt + 1) * 128], pbT[:].rearrange("p (k c) -> p k c", c=128))
                    for ec in range(8):
                        bank = gb()
                        for kk in range(8):
                            k.mm(bank[:, 0:256], wk_b[:, kk, ec * 128:(ec + 1) * 128], memT[:, kk, :], start=(kk == 0), stop=(kk == 7))
                        k.copy("act", KT[:, ec, :], bank[:, 0:256])
                    for mt in range(2):
                        for half in range(2):
                            bank = gb()
                            for kk in range(8):
                                k.mm(bank[:], memT[:, kk, mt * 128:(mt + 1) * 128], wv_b[:, kk, half * 512:(half + 1) * 512], start=(kk == 0), stop=(kk == 7))
                            k.copy("act", Vb[:, mt, half * 512:(half + 1) * 512], bank[:])
                    XB = 512
                    hTx = [xsb(f"hTx{i}", [128, 8, XB], BF16) for i in range(2)]
                    qT = xsb("qT", [128, 8, XB], BF16)
                    Pf = xsb("Pf", [128, 4, 256])
                    Pn = xsb("Pn", [128, 4, 256], BF16)
                    PTs = xsb("PTs", [128, 8, 128], BF16)
                    OTs = xsb("OTs", [128, 8, XB], BF16)
                    mx = xsb("mx", [128, 4])
                    nmx = xsb("nmx", [128, 4])
                    rs = xsb("rs", [128, 4])
                    rinv = xsb("rinv", [128, 4])
                    btx = xsb("btx", [128, 4, D])
                    xsmall = {"stats": xsb("x_stats", [128, 2, 6]), "mv": xsb("x_mv", [128, 2]),
                              "rstd": xsb("x_rstd", [128, 1]), "nmr": xsb("x_nmr", [128, 1])}
                    h2tb = xsb("h2tb", [128, D], BF16)
                    h2Tt = xsb("h2Tt", [128, 8, 128], BF16)
                    SC = 1.0 / 16.0
                    for xb_ in range(T // XB):
                        ht_ = hTx[xb_ % 2]
                        S.dma(ht_[:], h1T_d[:, :, xb_ * XB:(xb_ + 1) * XB])
                        for ec in range(8):
                            bank = gb()
                            for kk in range(8):
                                k.mm(bank[:], wq_b[:, kk, ec * 128:(ec + 1) * 128], ht_[:, kk, :], start=(kk == 0), stop=(kk == 7))
                            k.copy("act", qT[:, ec, :], bank[:])
                        for cc in range(XB // 128):
                            ci = xb_ * (XB // 128) + cc
                            o = cc * 128
                            for hd in range(4):
                                for e2 in range(2):
                                    k.mm(pA[:, hd * 256:(hd + 1) * 256], qT[:, 2 * hd + e2, o:o + 128], KT[:, 2 * hd + e2, :], start=(e2 == 0), stop=(e2 == 1))
                            S.op("dve", lambda e: e.reduce_max(mx[:], pA[:].rearrange("p (h m) -> p h m", h=4), mybir.AxisListType.X),
                                 reads=[pA[:]], writes=[mx[:]])
                            k.ts("dve", nmx[:], mx[:], -SC, None, ALU.mult)
                            for hd in range(4):
                                k.act(Pf[:, hd, :], pA[:, hd * 256:(hd + 1) * 256], AF.Exp, bias=nmx[:, hd:hd + 1], scale=SC, accum_out=rs[:, hd:hd + 1])
                            S.op("dve", lambda e: e.reciprocal(rinv[:], rs[:]), reads=[rs[:]], writes=[rinv[:]])
                            k.tt("pool", Pn[:], Pf[:], rinv[:].unsqueeze(2).to_broadcast([128, 4, 256]), ALU.mult)
                            for hd in range(4):
                                for mc in range(2):
                                    j = hd * 2 + mc
                                    k.tr(pbT[:, j * 128:(j + 1) * 128], Pn[:, hd, mc * 128:(mc + 1) * 128], ident[:], signal=(j == 7))
                            k.copy("dve", PTs[:], pbT[:].rearrange("p (k c) -> p k c", c=128))
                            for half in range(2):
                                bank = gb()
                                for jj in range(4):
                                    ec = half * 4 + jj
                                    hd = ec // 2
                                    for mc in range(2):
                                        k.mm(bank[:, jj * 128:(jj + 1) * 128], Vb[:, mc, ec * 128:(ec + 1) * 128], PTs[:, hd * 2 + mc, :], start=(mc == 0), stop=(mc == 1))
                                k.copy("act", OTs[:, half * 4:(half + 1) * 4, o:o + 128], bank[:].rearrange("p (j c) -> p j c", j=4))
                        for cc in range(XB // 128):
                            ci = xb_ * (XB // 128) + cc
                            o = cc * 128
                            S.dma(btx[:, 3, :], h1_d[ci * 128:(ci + 1) * 128, :])
                            banks = []
                            for half in range(2):
                                bank = gb()
                                for kk in range(8):
                                    k.mm(bank[:], OTs[:, kk, o:o + 128], wo_b[:, kk, half * 512:(half + 1) * 512], start=(kk == 0), stop=(kk == 7))
                                banks.append(bank)
                            ln_out_tile(xsmall, btx, btx[:, 3, :], banks, lng2, lnb2, h2_d, h2T_d, ci, h2tb, h2Tt)
                    S.barrier()
            if STAGE >= 3:
                with ExitStack() as fs:
                    def fsb(name, shape, dt=F32):
                        return sb(name, shape, dt, stack=fs)
                    w1_b = fsb("w1_b", [128, 8, 4 * D], BF16)
                    w2_b = fsb("w2_b", [128, 32, D], BF16)
                    w1v = w1.rearrange("(k p) n -> p k n", p=128)
                    w2v = w2.rearrange("(k p) n -> p k n", p=128)
                    for kk in range(8):
                        S.dma(w1_b[:, kk, :], w1v[:, kk, :], q="pool")
                    for kk in range(0, 32, 4):
                        S.dma(w2_b[:, kk:kk + 4, :], w2v[:, kk:kk + 4, :], q="pool")
                    lng3 = fsb("lng3", [128, D])
                    lnb3 = fsb("lnb3", [128, D])
                    S.dma(lng3[:], lnp[4].partition_broadcast(128))
                    S.dma(lnb3[:], lnp[5].partition_broadcast(128))
                    FB = 256
                    hTf = [fsb(f"hTf{i}", [128, 8, FB], BF16) for i in range(2)]
                    hid = fsb("hid", [128, 32, FB], BF16)
                    rl = [fsb(f"rl{i}", [128, FB]) for i in range(2)]
                    btf = fsb("btf", [128, 4, D])
                    fsmall = {"stats": fsb("f_stats", [128, 2, 6]), "mv": fsb("f_mv", [128, 2]),
                              "rstd": fsb("f_rstd", [128, 1]), "nmr": fsb("f_nmr", [128, 1])}
                    for fb_ in range(T // FB):
                        ht_ = hTf[fb_ % 2]
                        S.dma(ht_[:], h2T_d[:, :, fb_ * FB:(fb_ + 1) * FB])
                        for fc in range(32):
                            bank = gb()
                            for kk in range(8):
                                k.mm(bank[:, 0:FB], w1_b[:, kk, fc * 128:(fc + 1) * 128], ht_[:, kk, :], start=(kk == 0), stop=(kk == 7))
                            r_ = rl[fc % 2]
                            k.act(r_[:], bank[:, 0:FB], AF.Relu)
                            k.tt("pool" if fc % 2 == 0 else "dve", hid[:, fc, :], r_[:], r_[:], ALU.mult)
                        for cc in range(FB // 128):
                            ci = fb_ * (FB // 128) + cc
                            o = cc * 128
                            S.dma(btf[:, 3, :], h2_d[ci * 128:(ci + 1) * 128, :])
                            banks = []
                            for half in range(2):
                                bank = gb()
                                for fc in range(32):
                                    k.mm(bank[:], hid[:, fc, o:o + 128], w2_b[:, fc, half * 512:(half + 1) * 512], start=(fc == 0), stop=(fc == 31))
                                banks.append(bank)
                            ln_out_tile(fsmall, btf, btf[:, 3, :], banks, lng3, lnb3, out_h, None, ci, None, None)
                    S.barrier()
            src = {1: h1_d, 2: h2_d}.get(STAGE)
            Sched.stopped = False
            if src is not None:
                with ExitStack() as cs:
                    ct = [cs.enter_context(nc.sbuf_tensor(f"cpy{i}", [128, D], F32)) for i in range(2)]
                    for i in range(NCH):
                        S.dma(ct[i % 2][:], src[i * 128:(i + 1) * 128, :])
                        S.dma(out_h[i * 128:(i + 1) * 128, :], ct[i % 2][:])
                    S.barrier()
        S.finish()
    build_program.stats = (S.n_inst, S.n_wait)
    return nc


build_program.stage = 3
build_program.stop = 0


def layer_norm_tile(S, k, vv, stats, mv, rstd, nmr, xn, lng, lnb, outt):
    for half in range(2):
        S.op("dve", lambda e, half=half: e.bn_stats(stats[:, half, :], vv[:, half * 512:(half + 1) * 512]),
             reads=[vv[:, half * 512:(half + 1) * 512]], writes=[stats[:, half, :]])
    S.op("dve", lambda e: e.bn_aggr(mv[:], stats[:]), reads=[stats[:]], writes=[mv[:]])
    k.ts("dve", rstd[:], mv[:, 1:2], EPS, None, ALU.add)
    k.act(rstd[:], rstd[:], AF.Ln)
    k.act(rstd[:], rstd[:], AF.Exp, scale=-0.5)
    k.stt("dve", nmr[:], mv[:, 0:1], -1.0, rstd[:], ALU.mult, ALU.mult)
    k.act(xn[:], vv[:], AF.Identity, bias=nmr[:, 0:1], scale=rstd[:, 0:1])
    k.tt("pool", xn[:], xn[:], lng[:], ALU.mult)
    k.tt("pool", outt[:], xn[:], lnb[:], ALU.add)


def xattn_phase(nc, S, k, es, L):
    raise NotImplementedError


def mlp_phase(nc, S, k, es, L):
    raise NotImplementedError


def _consts():
    s = np.arange(128)[:, None]
    l = np.arange(128)[None, :]
    c = np.zeros((128, 4, 128), np.float32)
    c[:, 0, :] = np.eye(128)
    c[:, 1, :] = (s <= l)
    c[:, 2, :] = np.where(l >= s, 0.0, -1e30)
    c[:, 3, :] = np.broadcast_to(np.arange(128, dtype=np.float32), (128, 128))
    return c


def _pp(a, n):
    return np.ascontiguousarray(np.asarray(a, np.float32).reshape(n, 128).T)


def _prep_layer(I, l):
    f = np.float32
    d = {}
    d["w_in"] = np.ascontiguousarray(I["w_in"][l])
    d["cst"] = _consts()
    d["convw"] = np.ascontiguousarray(I["ssd_conv_w"][l].reshape(4, 8, 128).transpose(2, 1, 0))
    d["convb"] = _pp(I["ssd_conv_b"][l], 8)
    d["rowp"] = np.ascontiguousarray(np.stack([I["ssd_dt_bias"][l], I["ssd_a_log"][l], I["ssd_d"][l]]).astype(f))
    flat = [I["s5_lam_re"][l].reshape(1024), I["s5_lam_im"][l].reshape(1024), np.repeat(I["s5_log_step"][l], 64)]
    d["s5pp"] = np.ascontiguousarray(np.stack([_pp(a, 8) for a in flat], axis=1))
    d["s5row"] = np.ascontiguousarray(np.stack(flat).astype(f))
    bp = np.zeros((128, 2, 8, 128), f)
    cp = np.zeros((128, 2, 8, 128), f)
    for g in range(16):
        q = g // 2
        for ri, (bb, cc) in enumerate(((I["s5_b_re"][l], I["s5_c_re"][l]), (I["s5_b_im"][l], I["s5_c_im"][l]))):
            bp[(g % 8) * 16:(g % 8) * 16 + 16, ri, q, (g % 2) * 64:(g % 2) * 64 + 64] = bb[g].T
            cp[(g % 2) * 64:(g % 2) * 64 + 64, ri, q, (g % 8) * 16:(g % 8) * 16 + 16] = cc[g].T
    d["s5bpad"] = bp
    d["s5cpad"] = cp
    d["rgcw"] = np.ascontiguousarray(I["rg_conv_w"][l].reshape(4, 2, 128).transpose(2, 1, 0))
    d["rgp"] = np.ascontiguousarray(np.stack([_pp(I["rg_conv_b"][l], 2), _pp(I["rg_ba"][l].reshape(256), 2),
                                              _pp(I["rg_bx"][l].reshape(256), 2), _pp(I["rg_lambda"][l], 2)], axis=1))
    rw = np.zeros((128, 2, 2, 128), f)
    for gi, W in enumerate((I["rg_wa"][l], I["rg_wx"][l])):
        for h in range(4):
            rw[(h % 2) * 64:(h % 2) * 64 + 64, gi, h // 2, (h % 2) * 64:(h % 2) * 64 + 64] = W[h]
    d["rgw"] = rw
    d["w_out"] = np.ascontiguousarray(I["w_out"][l])
    d["normw"] = np.ascontiguousarray(I["ssd_norm_w"][l])
    d["s5dg"] = np.ascontiguousarray(np.stack([_pp(I["s5_d"][l], 2), _pp(I["s5_glu_b"][l], 2)], axis=1))
    d["gluw"] = np.ascontiguousarray(I["s5_glu_w"][l])
    d["lnp"] = np.ascontiguousarray(np.stack([I["ln1_g"][l], I["ln1_b"][l], I["ln2_g"][l], I["ln2_b"][l], I["ln3_g"][l], I["ln3_b"][l]]))
    for a, b in (("wq", "xa_wq"), ("wk", "xa_wk"), ("wv", "xa_wv"), ("wo", "xa_wo"), ("w1", "mlp_w1"), ("w2", "mlp_w2")):
        d[a] = np.ascontiguousarray(I[b][l])
    return d


_A_KEYS = ["w_in", "cst", "convw", "convb", "rowp", "s5pp", "s5row", "s5bpad", "rgcw", "rgp", "rgw"]
_PROGS = {}


def _prog(mode):
    if mode not in _PROGS:
        _PROGS[mode] = build_program(mode)
    return _PROGS[mode]


def kernel(**inputs):
    I = {k_: np.asarray(v) for k_, v in inputs.items()}
    x = np.ascontiguousarray(I["x"], dtype=np.float32)
    B, L, _ = x.shape
    ncores = 8
    per = L // 4
    h = x
    for l in range(DEPTH):
        P = _prep_layer(I, l)
        hs, halos = [], []
        for c in range(ncores):
            b, j = c // 4, c % 4
            hs.append(np.ascontiguousarray(h[b, j * per:(j + 1) * per]))
            halos.append(np.zeros((128, D), np.float32) if j == 0 else np.ascontiguousarray(h[b, j * per - 128:j * per]))
        mapsA = [dict({k_: P[k_] for k_ in _A_KEYS}, h=hs[c], halo=halos[c]) for c in range(ncores)]
        rA = run_bass_kernel_spmd(_prog("A"), mapsA, core_ids=list(range(ncores))).results
        mapsB = []
        for c in range(ncores):
            b, j = c // 4, c % 4
            p_ssd = np.zeros((3, 128, 512), np.float32)
            p_asum = np.zeros((3, 8), np.float32)
            p_s5 = np.zeros((3, 128, 2, 8), np.float32)
            p_rg = np.zeros((3, 128, 2, 2), np.float32)
            for i in range(3):
                src = j - 1 - i
                if src >= 0:
                    r = rA[b * 4 + src]
                    p_ssd[i] = r["o_ssd"]
                    p_asum[i] = r["o_asum"][0]
                    p_s5[i] = r["o_s5"]
                    p_rg[i] = r["o_rg"]
            m = dict(P)
            m.update(h=hs[c], halo=halos[c], mem=np.ascontiguousarray(I["mem"][b], dtype=np.float32),
                     p_ssd=p_ssd, p_asum=p_asum, p_s5=p_s5, p_rg=p_rg)
            mapsB.append(m)
        rB = run_bass_kernel_spmd(_prog("B"), mapsB, core_ids=list(range(ncores))).results
        hn = np.empty_like(h)
        for c in range(ncores):
            b, j = c // 4, c % 4
            hn[b, j * per:(j + 1) * per] = rB[c]["out_h"]
        h = hn
    return h.astype(I["x"].dtype)
```
